# Optimizing a Trainium2 kernel written in Bass

```python
import math
import jax, jax.numpy as jnp
from jax import lax
import numpy as np

D_MODEL = 4096
BATCH = 2
SEQ = 4096
DEPTH = 2

N_META = 16
ATTN_HEADS = 16
HEAD_DIM = 128
D_ATTN = ATTN_HEADS * HEAD_DIM
D_CONV = D_MODEL - D_ATTN
CONV_GROUPS = 16
CONV_WIDTH = 3
D_MIX = D_ATTN + D_CONV
D_IN = 4 * D_ATTN + 4 * D_CONV
Q_BLOCK = 128
EPS = 1e-6

kernel_name = "hymba_stickbreak_shortconv_hybrid"


def rmsnorm(x, g):
    xf = x.astype(jnp.float32)
    y = xf * lax.rsqrt(jnp.mean(xf * xf, axis=-1, keepdims=True) + EPS)
    return (y * g.astype(jnp.float32)).astype(x.dtype)


def group_rmsnorm(x, g, groups):
    shp = x.shape
    xf = x.astype(jnp.float32).reshape(shp[:-1] + (groups, shp[-1] // groups))
    y = xf * lax.rsqrt(jnp.mean(xf * xf, axis=-1, keepdims=True) + EPS)
    return (y.reshape(shp) * g.astype(jnp.float32)).astype(x.dtype)


def stick_breaking_attention(q, k, v):
    L = q.shape[2]
    scale = 1.0 / math.sqrt(HEAD_DIM)
    bounds = [(0, min(N_META, L))] + [(s, min(s + Q_BLOCK, L)) for s in range(N_META, L, Q_BLOCK)]
    outs = []
    for q0, q1 in bounds:
        qb = q[:, :, q0:q1].astype(jnp.float32)
        kb = k[:, :, :q1].astype(jnp.float32)
        vb = v[:, :, :q1].astype(jnp.float32)
        z = jnp.einsum('bhqd,bhkd->bhqk', qb, kb) * scale
        t_pos = jnp.arange(q0, q1)[:, None]
        s_pos = jnp.arange(q1)[None, :]
        mask = s_pos < t_pos
        log_beta = jax.nn.log_sigmoid(z)
        log_keep = jnp.where(mask, jax.nn.log_sigmoid(-z), 0.0)
        after = lax.cumsum(log_keep, axis=3, reverse=True) - log_keep
        a = jnp.where(mask, jnp.exp(log_beta + after), 0.0)
        outs.append(jnp.einsum('bhqk,bhkd->bhqd', a, vb))
    return jnp.concatenate(outs, axis=2).astype(v.dtype)


def short_conv(u, w):
    L = u.shape[1]
    up = jnp.pad(u, ((0, 0), (CONV_WIDTH - 1, 0), (0, 0)))
    y = up[:, 0:L] * w[0]
    for i in range(1, CONV_WIDTH):
        y = y + up[:, i:i + L] * w[i]
    return y


def hybrid_layer(x, norm_g, w_in, conv_w, attn_norm_g, conv_norm_g, w_out):
    B, L, _ = x.shape
    h = rmsnorm(x, norm_g)
    p = h @ w_in
    cuts = [D_ATTN, 2 * D_ATTN, 3 * D_ATTN, 4 * D_ATTN,
            4 * D_ATTN + D_CONV, 4 * D_ATTN + 2 * D_CONV, 4 * D_ATTN + 3 * D_CONV]
    q, k, v, g_attn, b_conv, c_conv, h_conv, z_conv = jnp.split(p, cuts, axis=-1)

    def heads(t):
        return t.reshape(B, L, ATTN_HEADS, HEAD_DIM).transpose(0, 2, 1, 3)
    o = stick_breaking_attention(heads(q), heads(k), heads(v))
    o = o.transpose(0, 2, 1, 3).reshape(B, L, D_ATTN)
    o = group_rmsnorm(o * jax.nn.silu(g_attn), attn_norm_g, ATTN_HEADS)

    y = b_conv * short_conv(c_conv * h_conv, conv_w)
    y = group_rmsnorm(y * jax.nn.silu(z_conv), conv_norm_g, CONV_GROUPS)

    return x + jnp.concatenate([o, y], axis=-1) @ w_out


def setup_inputs(seed: int = 0) -> dict:
    key = jax.random.key(seed)
    ks = jax.random.split(key, 10)
    f32 = jnp.float32
    x = jax.random.normal(ks[0], (BATCH, SEQ, D_MODEL), f32)
    meta_tokens = jax.random.normal(ks[1], (N_META, D_MODEL), f32)
    norm_g = 1.0 + 0.01 * jax.random.normal(ks[2], (DEPTH, D_MODEL), f32)
    w_in = jax.random.normal(ks[3], (DEPTH, D_MODEL, D_IN), f32) * (D_MODEL ** -0.5)
    conv_w = jax.random.normal(ks[4], (DEPTH, CONV_WIDTH, D_CONV), f32) * (CONV_WIDTH ** -0.5)
    attn_norm_g = 1.0 + 0.01 * jax.random.normal(ks[5], (DEPTH, D_ATTN), f32)
    conv_norm_g = 1.0 + 0.01 * jax.random.normal(ks[6], (DEPTH, D_CONV), f32)
    w_out = jax.random.normal(ks[7], (DEPTH, D_MIX, D_MODEL), f32) * (D_MIX ** -0.5)
    final_norm_g = 1.0 + 0.01 * jax.random.normal(ks[8], (D_MODEL,), f32)
    return {"x": x, "meta_tokens": meta_tokens, "norm_g": norm_g, "w_in": w_in,
            "conv_w": conv_w, "attn_norm_g": attn_norm_g, "conv_norm_g": conv_norm_g,
            "w_out": w_out, "final_norm_g": final_norm_g}


def reference(x, meta_tokens, norm_g, w_in, conv_w, attn_norm_g, conv_norm_g, w_out, final_norm_g):
    B = x.shape[0]
    meta = jnp.broadcast_to(meta_tokens.astype(x.dtype)[None], (B, N_META, D_MODEL))
    hs = jnp.concatenate([meta, x], axis=1)
    for l in range(DEPTH):
        hs = hybrid_layer(hs, norm_g[l], w_in[l], conv_w[l], attn_norm_g[l],
                          conv_norm_g[l], w_out[l])
    return rmsnorm(hs, final_norm_g)[:, N_META:]
```

```python
import math
from contextlib import ExitStack

import numpy as np
import ml_dtypes

import concourse.bass as bass
import concourse.mybir as mybir
from concourse.bass_utils import run_bass_kernel_spmd

F32 = mybir.dt.float32
BF16 = mybir.dt.bfloat16
AF = mybir.ActivationFunctionType
ALU = mybir.AluOpType
NPBF = ml_dtypes.bfloat16

D = 4096
NBATCH = 2
SEQ = 4096
NMETA = 16
L = SEQ + NMETA
LP = 4224
LT = NBATCH * LP
NBLK = LP // 128
NCORES = 8
FPC = D // NCORES
EPS = 1e-6
SCALE = 1.0 / math.sqrt(128.0)
NCH = D // 128
TT = 256


class Buf:
    __slots__ = ("w", "r", "name")

    def __init__(self, name=""):
        self.w = None
        self.r = {}
        self.name = name


class Prog:
    ENGS = ("pe", "act", "dve", "pool", "sp")

    def __init__(self, nc):
        self.nc = nc
        self.streams = {e: [] for e in self.ENGS}
        self.cnt = {}

    def op(self, eng, fn, reads=(), writes=(), dma=None):
        waits = {}

        def need(tok):
            if tok is None:
                return
            k, v = tok
            if waits.get(k, 0) < v:
                waits[k] = v

        for b in reads:
            need(b.w)
        for b in writes:
            need(b.w)
            for k, v in b.r.items():
                need((k, v))
        key = dma if dma is not None else eng
        inc = 16 if dma is not None else 1
        self.cnt[key] = self.cnt.get(key, 0) + inc
        tok = (key, self.cnt[key])
        self.streams[eng].append((waits, fn, key, inc))
        for b in writes:
            b.w = tok
            b.r = {}
        for b in reads:
            if b.r.get(key, 0) < tok[1]:
                b.r[key] = tok[1]
        return tok

    def barrier(self):
        snap = dict(self.cnt)
        for e in self.ENGS:
            self.streams[e].append((dict(snap), None, None, 0))

    def build(self):
        nc = self.nc
        with ExitStack() as es:
            sems = {k: es.enter_context(nc.semaphore("s_" + k)) for k in self.cnt}
            block = es.enter_context(nc.Block())

            def mk(ename):
                def body(e):
                    known = {}
                    for waits, fn, key, inc in self.streams[ename]:
                        for k, v in waits.items():
                            if known.get(k, 0) < v:
                                e.wait_ge(sems[k], v)
                                known[k] = v
                        if fn is not None:
                            ins = fn(e)
                            ins.then_inc(sems[key], inc)
                return body

            block.tensor(mk("pe"))
            block.scalar(mk("act"))
            block.vector(mk("dve"))
            block.gpsimd(mk("pool"))
            block.sync(mk("sp"))


def _tiles(total, step):
    return [(t0, min(step, total - t0)) for t0 in range(0, total, step)]


class MixerB:
    def __init__(self, nc, p, es, hT, w_att, w_conv, convw, gains, cst_bf, cst_f32, mix_out, sg_dram):
        self.nc, self.p = nc, p
        self.hT = hT
        self.w_att = w_att
        self.w_conv = w_conv
        self.mix_out = mix_out
        self.sg_dram = sg_dram
        sb = lambda name, shape, dt: es.enter_context(nc.sbuf_tensor(name, shape, dt))
        ps = lambda name: es.enter_context(nc.psum_tensor(name, [128, 512], F32))
        self.W = sb("Wbuf", [128, NCH, 1024], BF16)
        self.hs = sb("hslot", [128, 2, NCH, TT], BF16)
        self.stg = sb("wstg", [128, 2, 1024], F32)
        self.qT = sb("qT", [128, 2, LP], BF16)
        self.kT = sb("kT", [128, 2, LP], BF16)
        self.v = sb("vtok", [128, NBLK, 256], BF16)
        self.e_t = sb("e_t", [128, 2, 512], F32)
        self.sp_t = sb("sp_t", [128, 2, 512], F32)
        self.lk_t = sb("lk_t", [128, 4, 512], BF16)
        self.at_t = sb("at_t", [128, 4, 512], BF16)
        self.S_t = sb("S_t", [128, 3, 512], BF16)
        self.og = sb("og", [128, 512], F32)
        self.sq = sb("sq", [128, 512], F32)
        self.rs = sb("rs", [128, 512], F32)
        self.mo = sb("mo", [128, 2, 512], BF16)
        self.sgl = sb("sgl", [128, 2, 512], BF16)
        self.sgw = sb("sgw", [128, 2, TT], BF16)
        self.tmpa = sb("tmpa", [128, 2, TT], F32)
        self.tmpb = sb("tmpb", [128, 2, TT], F32)
        self.u_t = sb("u_t", [128, 2, TT + 2], F32)
        self.y_t = sb("y_t", [128, TT], F32)
        self.cbf = sb("cbf", [128, 4, 128], BF16)
        self.cf32 = sb("cf32", [128, 128], F32)
        self.cw = sb("cw", [128, 6], F32)
        self.gn = sb("gn", [128, 4], F32)
        self.epsc = sb("epsc", [128, 1], F32)
        self.banks = [ps("bank%d" % i) for i in range(8)]
        self.bb = [Buf("bank%d" % i) for i in range(8)]
        self.cst_bf, self.cst_f32, self.convw, self.gains = cst_bf, cst_f32, convw, gains
        self.b_hs = [Buf("hs0"), Buf("hs1")]
        self.b_stg = [Buf("stg0"), Buf("stg1")]
        self.b_W = Buf("W")
        self.b_e = [Buf(), Buf()]
        self.b_sp = [Buf(), Buf()]
        self.b_lk = [Buf() for _ in range(4)]
        self.b_at = [Buf() for _ in range(4)]
        self.b_S = [Buf() for _ in range(3)]
        self.b_og, self.b_sq, self.b_rs = Buf(), Buf(), Buf()
        self.b_mo = [Buf(), Buf()]
        self.b_sgl = [Buf(), Buf()]
        self.b_sgw = [Buf(), Buf()]
        self.b_ta = [Buf(), Buf()]
        self.b_tb = [Buf(), Buf()]
        self.b_u = [Buf(), Buf()]
        self.b_y = Buf()
        self.pj_rr = 0
        self.ev_rr = 0
        self.wstage = 0

    def load_consts(self):
        p = self.p
        p.op("sp", lambda e: e.dma_start(out=self.cbf[:], in_=self.cst_bf), dma="c0")
        p.op("sp", lambda e: e.dma_start(out=self.cf32[:], in_=self.cst_f32), dma="c0")
        p.op("sp", lambda e: e.dma_start(out=self.cw[:], in_=self.convw), dma="c0")
        p.op("sp", lambda e: e.dma_start(out=self.gn[:], in_=self.gains), dma="c0")
        p.op("pool", lambda e: e.memset(self.epsc[:], EPS))

    def w_stage_ops(self, w_src, s):
        p = self.p
        stg = self.stg
        for j in range(2):
            sl = self.wstage % 2
            self.wstage += 1
            b = self.b_stg[sl]
            c = 2 * s + j
            srcj = w_src.rearrange("(c q) n -> q c n", q=128)[:, c, :]
            p.op("sp", lambda e, sl=sl, srcj=srcj: e.dma_start(out=stg[:, sl, :], in_=srcj),
                 writes=[b], dma="wst%d" % sl)
            p.op("pool", lambda e, sl=sl, c=c: e.tensor_copy(out=self.W[:, c, :], in_=stg[:, sl, :]),
                 reads=[b], writes=[self.b_W])

    def load_w_all(self, w_src):
        for s in range(NCH // 2):
            self.w_stage_ops(w_src, s)

    def load_h(self, slot, col0, tw):
        p = self.p
        src = self.hT.rearrange("(c q) t -> q c t", q=128)
        q4 = NCH // 4
        for j in range(4):
            p.op("sp", lambda e, j=j: e.dma_start(out=self.hs[:, slot, q4 * j:q4 * j + q4, 0:tw],
                                                    in_=src[:, q4 * j:q4 * j + q4, col0:col0 + tw]),
                 writes=[self.b_hs[slot]], dma="hs%d" % slot)

    def next_bank(self, lo=0, n=4):
        i = lo + (self.pj_rr % n)
        self.pj_rr += 1
        return i

    def proj_fm(self, slot, wc0, tw, bank):
        def fn(e):
            ins = None
            for c in range(NCH):
                ins = e.matmul(self.banks[bank][:, 0:tw], lhsT=self.W[:, c, wc0:wc0 + 128],
                               rhs=self.hs[:, slot, c, 0:tw], start=(c == 0), stop=(c == NCH - 1))
            return ins
        self.p.op("pe", fn, reads=[self.b_W, self.b_hs[slot]], writes=[self.bb[bank]])

    def phase_proj_att(self, b):
        p = self.p
        tiles = _tiles(LP, TT)
        self.load_h(0, b * LP, tiles[0][1])
        for ti, (t0, tw) in enumerate(tiles):
            slot = ti % 2
            if ti + 1 < len(tiles):
                self.load_h((ti + 1) % 2, b * LP + tiles[ti + 1][0], tiles[ti + 1][1])
            self._att_tile(b, t0, tw, slot)

    def _att_tile(self, b, t0, tw, slot):
        p = self.p
        if True:
            for h in range(2):
                bk = self.next_bank()
                self.proj_fm(slot, 0 + 128 * h, tw, bk)
                p.op("act", lambda e, bk=bk, h=h: e.activation(out=self.qT[:, h, t0:t0 + tw],
                                                              in_=self.banks[bk][:, 0:tw], func=AF.Copy),
                     reads=[self.bb[bk]])
                bk = self.next_bank()
                self.proj_fm(slot, 256 + 128 * h, tw, bk)
                p.op("dve", lambda e, bk=bk, h=h: e.tensor_scalar(out=self.kT[:, h, t0:t0 + tw],
                                                                 in0=self.banks[bk][:, 0:tw], scalar1=SCALE,
                                                                 scalar2=None, op0=ALU.mult),
                     reads=[self.bb[bk]])
                bk = self.next_bank()
                self.proj_fm(slot, 768 + 128 * h, tw, bk)
                ts = self.ev_rr % 2
                self.ev_rr += 1
                p.op("act", lambda e, bk=bk, ts=ts: e.activation(out=self.tmpa[:, ts, 0:tw], in_=self.banks[bk][:, 0:tw],
                                                                func=AF.Exp, scale=-1.0),
                     reads=[self.bb[bk]], writes=[self.b_ta[ts]])
                p.op("dve", lambda e, ts=ts: e.tensor_scalar(out=self.tmpa[:, ts, 0:tw], in0=self.tmpa[:, ts, 0:tw],
                                                            scalar1=1.0, scalar2=None, op0=ALU.add),
                     reads=[], writes=[self.b_ta[ts]])
                p.op("dve", lambda e, ts=ts: e.reciprocal(out=self.tmpb[:, ts, 0:tw], in_=self.tmpa[:, ts, 0:tw]),
                     reads=[self.b_ta[ts]], writes=[self.b_tb[ts]])
                p.op("dve", lambda e, bk=bk, ts=ts: e.tensor_tensor(out=self.sgw[:, ts, 0:tw], in0=self.banks[bk][:, 0:tw],
                                                                   in1=self.tmpb[:, ts, 0:tw], op=ALU.mult),
                     reads=[self.bb[bk], self.b_tb[ts]], writes=[self.b_sgw[ts]])
                p.op("sp", lambda e, ts=ts, h=h: e.dma_start(
                    out=self.sg_dram[128 * h:128 * h + 128, b * LP + t0:b * LP + t0 + tw], in_=self.sgw[:, ts, 0:tw]),
                    reads=[self.b_sgw[ts]], dma="sgw%d" % ts)
            for j in range(tw // 128):
                bk = self.next_bank()

                def fn(e, j=j, bk=bk):
                    ins = None
                    for c in range(NCH):
                        ins = e.matmul(self.banks[bk][:, 0:256], lhsT=self.hs[:, slot, c, 128 * j:128 * j + 128],
                                       rhs=self.W[:, c, 512:768], start=(c == 0), stop=(c == NCH - 1))
                    return ins
                p.op("pe", fn, reads=[self.b_W, self.b_hs[slot]], writes=[self.bb[bk]])
                blk = (t0 + 128 * j) // 128
                p.op("act", lambda e, bk=bk, blk=blk: e.activation(out=self.v[:, blk, :], in_=self.banks[bk][:, 0:256],
                                                                  func=AF.Copy),
                     reads=[self.bb[bk]])

    def phase_attention(self, b, wload=None):
        p = self.p
        tiles = []
        for h in range(2):
            for g, (g0, gw) in enumerate(_tiles(LP, 512)):
                kbs = list(range((g0 + gw) // 128 - 1, -1, -1))
                for n, kb in enumerate(kbs):
                    c0 = max(0, kb * 128 - g0)
                    tiles.append(dict(h=h, g=g, g0=g0, gw=gw, kb=kb, c0=c0, first=(n == 0), last=(kb == 0),
                                      diag=(kb * 128 >= g0), gi=h * 9 + g))
        n = len(tiles)
        Z0, X0, O0, MS = 0, 2, 4, 6
        wl = 0
        for i in range(n + 4):
            if i < n:
                t = tiles[i]
                zs, es_, ls = i % 2, i % 2, i % 4
                h, kb, g0, c0, gw = t["h"], t["kb"], t["g0"], t["c0"], t["gw"]
                zb = Z0 + zs
                p.op("pe", lambda e, zb=zb, h=h, kb=kb, g0=g0, c0=c0, gw=gw: e.matmul(
                    self.banks[zb][:, c0:gw], lhsT=self.kT[:, h, 128 * kb:128 * kb + 128],
                    rhs=self.qT[:, h, g0 + c0:g0 + gw], start=True, stop=True),
                    writes=[self.bb[zb]])
                p.op("act", lambda e, zb=zb, es_=es_, c0=c0, gw=gw: e.activation(
                    out=self.e_t[:, es_, c0:gw], in_=self.banks[zb][:, c0:gw], func=AF.Exp, scale=-1.0),
                    reads=[self.bb[zb]], writes=[self.b_e[es_]])
            if 0 <= i - 2 < n:
                t = tiles[i - 2]
                j = i - 2
                xs, ls, ats = j % 2, j % 4, j % 4
                h, kb, g0, c0, gw = t["h"], t["kb"], t["g0"], t["c0"], t["gw"]
                xb = X0 + xs
                scur = t["sidx"] = (0 if t["first"] else tiles[j - 1]["snext"])
                snext = t["snext"] = (scur + 1) % 3

                def fx(e, xb=xb, ls=ls, h=h, kb=kb, g0=g0, c0=c0, gw=gw, first=t["first"], scur=scur):
                    e.matmul(self.banks[xb][:, c0:gw], lhsT=self.cbf[:, 0, :], rhs=self.lk_t[:, ls, c0:gw],
                             start=True, stop=False)
                    if not first:
                        e.matmul(self.banks[xb][:, c0:gw], lhsT=self.cbf[:, 1, :], rhs=self.S_t[:, scur, c0:gw],
                                 start=False, stop=False)
                    return e.matmul(self.banks[xb][:, c0:gw], lhsT=self.kT[:, h, 128 * kb:128 * kb + 128],
                                    rhs=self.qT[:, h, g0 + c0:g0 + gw], start=False, stop=True)
                rd = [self.b_lk[ls]] + ([] if t["first"] else [self.b_S[scur]])
                p.op("pe", fx, reads=rd, writes=[self.bb[xb]])
                if not t["last"]:
                    if t["first"]:
                        if c0 > 0:
                            p.op("pool", lambda e, snext=snext, c0=c0: e.memset(self.S_t[:, snext, 0:c0], 0.0),
                                 writes=[self.b_S[snext]])
                        p.op("pool", lambda e, snext=snext, ls=ls, c0=c0, gw=gw: e.tensor_copy(
                            out=self.S_t[:, snext, c0:gw], in_=self.lk_t[:, ls, c0:gw]),
                            reads=[self.b_lk[ls]], writes=[self.b_S[snext]])
                    else:
                        if c0 > 0:
                            p.op("pool", lambda e, snext=snext, scur=scur, c0=c0: e.tensor_copy(
                                out=self.S_t[:, snext, 0:c0], in_=self.S_t[:, scur, 0:c0]),
                                reads=[self.b_S[scur]], writes=[self.b_S[snext]])
                        p.op("pool", lambda e, snext=snext, scur=scur, ls=ls, c0=c0, gw=gw: e.tensor_tensor(
                            out=self.S_t[:, snext, c0:gw], in0=self.S_t[:, scur, c0:gw], in1=self.lk_t[:, ls, c0:gw],
                            op=ALU.add),
                            reads=[self.b_S[scur], self.b_lk[ls]], writes=[self.b_S[snext]])
                p.op("act", lambda e, xb=xb, ats=ats, c0=c0, gw=gw: e.activation(
                    out=self.at_t[:, ats, c0:gw], in_=self.banks[xb][:, c0:gw], func=AF.Exp),
                    reads=[self.bb[xb]], writes=[self.b_at[ats]])
                if t["diag"]:
                    p.op("pool", lambda e, ats=ats, c0=c0: e.tensor_tensor(
                        out=self.at_t[:, ats, c0:c0 + 128], in0=self.at_t[:, ats, c0:c0 + 128],
                        in1=self.cbf[:, 2, :], op=ALU.mult),
                        reads=[self.b_at[ats]], writes=[self.b_at[ats]])
            if i < n:
                t = tiles[i]
                zs, es_, ls = i % 2, i % 2, i % 4
                c0, gw = t["c0"], t["gw"]
                zb = Z0 + zs
                p.op("act", lambda e, es_=es_, c0=c0, gw=gw: e.activation(
                    out=self.sp_t[:, es_, c0:gw], in_=self.e_t[:, es_, c0:gw], func=AF.Ln, bias=1.0),
                    reads=[self.b_e[es_]], writes=[self.b_sp[es_]])
                p.op("dve", lambda e, zb=zb, es_=es_, ls=ls, c0=c0, gw=gw: e.scalar_tensor_tensor(
                    out=self.lk_t[:, ls, c0:gw], in0=self.banks[zb][:, c0:gw], scalar=1.0,
                    in1=self.sp_t[:, es_, c0:gw], op0=ALU.mult, op1=ALU.add),
                    reads=[self.bb[zb], self.b_sp[es_]], writes=[self.b_lk[ls]])
                if t["diag"]:
                    p.op("dve", lambda e, ls=ls, c0=c0: e.tensor_tensor(
                        out=self.lk_t[:, ls, c0:c0 + 128], in0=self.lk_t[:, ls, c0:c0 + 128],
                        in1=self.cbf[:, 2, :], op=ALU.mult),
                        reads=[self.b_lk[ls]], writes=[self.b_lk[ls]])
            if 0 <= i - 4 < n:
                t = tiles[i - 4]
                j = i - 4
                ats = j % 4
                h, kb, g0, c0, gw = t["h"], t["kb"], t["g0"], t["c0"], t["gw"]
                ob = O0 + (t["gi"] % 2)

                def fo(e, ob=ob, ats=ats, h=h, kb=kb, g0=g0, c0=c0, gw=gw, first=t["first"], last=t["last"]):
                    if first:
                        e.matmul(self.banks[ob][:, 0:gw], lhsT=self.cbf[:, 3, :], rhs=self.qT[:, h, g0:g0 + gw],
                                 start=True, stop=False)
                    return e.matmul(self.banks[ob][:, c0:gw], lhsT=self.v[:, kb, 128 * h:128 * h + 128],
                                    rhs=self.at_t[:, ats, c0:gw], start=False, stop=last)
                p.op("pe", fo, reads=[self.b_at[ats]], writes=[self.bb[ob]])
                if t["first"]:
                    sl = t["gi"] % 2
                    p.op("sp", lambda e, sl=sl, h=h, g0=g0, gw=gw: e.dma_start(
                        out=self.sgl[:, sl, 0:gw], in_=self.sg_dram[128 * h:128 * h + 128, b * LP + g0:b * LP + g0 + gw]),
                        writes=[self.b_sgl[sl]], dma="sgl%d" % sl)
                if t["last"]:
                    self.post_group(b, h, g0, gw, ob, t["gi"] % 2, MS)
                if wload is not None and (j % 12 == 5) and wl < NCH // 2:
                    self.w_stage_ops(wload, wl)
                    wl += 1
        if wload is not None:
            while wl < NCH // 2:
                self.w_stage_ops(wload, wl)
                wl += 1

    def norm_and_store(self, src_ap, src_bufs, gw, gain_col, row0, col0, ms_bank, sl, src_is_psum=False):
        p = self.p
        p.op("pool", lambda e: e.tensor_tensor(out=self.sq[:, 0:gw], in0=src_ap, in1=src_ap, op=ALU.mult),
             reads=src_bufs, writes=[self.b_sq])
        p.op("pe", lambda e: e.matmul(self.banks[ms_bank][:, 0:gw], lhsT=self.cf32[:, :], rhs=self.sq[:, 0:gw],
                                      start=True, stop=True),
             reads=[self.b_sq], writes=[self.bb[ms_bank]])
        p.op("act", lambda e: e.activation(out=self.rs[:, 0:gw], in_=self.banks[ms_bank][:, 0:gw], func=AF.Ln,
                                           bias=self.epsc[:, 0:1]),
             reads=[self.bb[ms_bank]], writes=[self.b_rs])
        p.op("act", lambda e: e.activation(out=self.rs[:, 0:gw], in_=self.rs[:, 0:gw], func=AF.Exp, scale=-0.5),
             reads=[], writes=[self.b_rs])
        p.op("dve", lambda e: e.scalar_tensor_tensor(out=self.mo[:, sl, 0:gw], in0=src_ap,
                                                     scalar=self.gn[:, gain_col:gain_col + 1], in1=self.rs[:, 0:gw],
                                                     op0=ALU.mult, op1=ALU.mult),
             reads=list(src_bufs) + [self.b_rs], writes=[self.b_mo[sl]])
        p.op("sp", lambda e: e.dma_start(out=self.mix_out[row0:row0 + 128, col0:col0 + gw], in_=self.mo[:, sl, 0:gw]),
             reads=[self.b_mo[sl]], dma="mo%d" % sl)

    def post_group(self, b, h, g0, gw, ob, sl, MS):
        p = self.p
        p.op("dve", lambda e: e.tensor_tensor(out=self.og[:, 0:gw], in0=self.banks[ob][:, 0:gw],
                                              in1=self.sgl[:, sl, 0:gw], op=ALU.mult),
             reads=[self.bb[ob], self.b_sgl[sl]], writes=[self.b_og])
        self.norm_and_store(self.og[:, 0:gw], [self.b_og], gw, h, 128 * h, b * LP + g0, MS, sl)

    def phase_conv(self, b):
        p = self.p
        tiles = _tiles(LP, TT)
        self.load_h(0, b * LP, tiles[0][1])
        for gi in range(2):
            p.op("pool", lambda e, gi=gi: e.memset(self.u_t[:, gi, 0:2], 0.0), writes=[self.b_u[gi]])
        MS = 7
        for ti, (t0, tw) in enumerate(tiles):
            slot = ti % 2
            if ti + 1 < len(tiles):
                self.load_h((ti + 1) % 2, b * LP + tiles[ti + 1][0], tiles[ti + 1][1])
            self._conv_tile(b, t0, tw, slot, MS)

    def _conv_tile(self, b, t0, tw, slot, MS):
        p = self.p
        if True:
            for gi in range(2):
                bB, bC, bH, bZ = [self.next_bank(0, 6) for _ in range(4)]
                self.proj_fm(slot, 0 + 128 * gi, tw, bB)
                self.proj_fm(slot, 256 + 128 * gi, tw, bC)
                self.proj_fm(slot, 512 + 128 * gi, tw, bH)
                self.proj_fm(slot, 768 + 128 * gi, tw, bZ)
                ts = self.ev_rr % 2
                self.ev_rr += 1
                p.op("act", lambda e, ts=ts, bH=bH: e.activation(out=self.tmpa[:, ts, 0:tw], in_=self.banks[bH][:, 0:tw],
                                                                func=AF.Copy),
                     reads=[self.bb[bH]], writes=[self.b_ta[ts]])
                p.op("act", lambda e, ts=ts, bZ=bZ: e.activation(out=self.tmpb[:, ts, 0:tw], in_=self.banks[bZ][:, 0:tw],
                                                                func=AF.Exp, scale=-1.0),
                     reads=[self.bb[bZ]], writes=[self.b_tb[ts]])
                p.op("dve", lambda e, ts=ts, bC=bC, gi=gi: e.tensor_tensor(out=self.u_t[:, gi, 2:2 + tw],
                                                                          in0=self.banks[bC][:, 0:tw],
                                                                          in1=self.tmpa[:, ts, 0:tw], op=ALU.mult),
                     reads=[self.bb[bC], self.b_ta[ts]], writes=[self.b_u[gi]])
                p.op("dve", lambda e, gi=gi: e.tensor_scalar(out=self.y_t[:, 0:tw], in0=self.u_t[:, gi, 0:tw],
                                                            scalar1=self.cw[:, 3 * gi:3 * gi + 1], scalar2=None,
                                                            op0=ALU.mult),
                     reads=[self.b_u[gi]], writes=[self.b_y])
                for tap in (1, 2):
                    p.op("dve", lambda e, gi=gi, tap=tap: e.scalar_tensor_tensor(
                        out=self.y_t[:, 0:tw], in0=self.u_t[:, gi, tap:tap + tw],
                        scalar=self.cw[:, 3 * gi + tap:3 * gi + tap + 1], in1=self.y_t[:, 0:tw],
                        op0=ALU.mult, op1=ALU.add),
                        reads=[self.b_u[gi]], writes=[self.b_y])
                p.op("dve", lambda e, gi=gi: e.tensor_copy(out=self.u_t[:, gi, 0:2], in_=self.u_t[:, gi, tw:tw + 2]),
                     reads=[self.b_u[gi]], writes=[self.b_u[gi]])
                p.op("dve", lambda e, bB=bB: e.tensor_tensor(out=self.y_t[:, 0:tw], in0=self.banks[bB][:, 0:tw],
                                                            in1=self.y_t[:, 0:tw], op=ALU.mult),
                     reads=[self.bb[bB]], writes=[self.b_y])
                p.op("dve", lambda e, ts=ts: e.tensor_scalar(out=self.tmpb[:, ts, 0:tw], in0=self.tmpb[:, ts, 0:tw],
                                                            scalar1=1.0, scalar2=None, op0=ALU.add),
                     reads=[], writes=[self.b_tb[ts]])
                p.op("dve", lambda e, ts=ts: e.reciprocal(out=self.tmpb[:, ts, 0:tw], in_=self.tmpb[:, ts, 0:tw]),
                     reads=[], writes=[self.b_tb[ts]])
                p.op("dve", lambda e, bZ=bZ: e.tensor_tensor(out=self.y_t[:, 0:tw], in0=self.banks[bZ][:, 0:tw],
                                                            in1=self.y_t[:, 0:tw], op=ALU.mult),
                     reads=[self.bb[bZ]], writes=[self.b_y])
                p.op("dve", lambda e, ts=ts: e.tensor_tensor(out=self.og[:, 0:tw], in0=self.y_t[:, 0:tw],
                                                            in1=self.tmpb[:, ts, 0:tw], op=ALU.mult),
                     reads=[self.b_y, self.b_tb[ts]], writes=[self.b_og])
                sl = self.ev_rr % 2
                self.norm_and_store(self.og[:, 0:tw], [self.b_og], tw, 2 + gi, 256 + 128 * gi, b * LP + t0, MS, sl)


def _const_arrays():
    j = np.arange(128)[:, None]
    s = np.arange(128)[None, :]
    mincN = -(j >= s).astype(np.float32)
    onesN = -np.ones((128, 128), np.float32)
    tri = (j < s).astype(np.float32)
    zer = np.zeros((128, 128), np.float32)
    cbf = np.stack([mincN, onesN, tri, zer], axis=1).astype(NPBF)
    cf32 = np.full((128, 128), 1.0 / 128.0, np.float32)
    return cbf, cf32


def build_B():
    nc = bass.Bass("TRN2", target_bir_lowering=False)
    hT = nc.dram_tensor("hT", [D, LT], BF16, kind="ExternalInput").ap()
    w_att = nc.dram_tensor("w_att", [D, 1024], F32, kind="ExternalInput").ap()
    w_conv = nc.dram_tensor("w_conv", [D, 1024], F32, kind="ExternalInput").ap()
    convw = nc.dram_tensor("convw", [128, 6], F32, kind="ExternalInput").ap()
    gains = nc.dram_tensor("gains", [128, 4], F32, kind="ExternalInput").ap()
    cst_bf = nc.dram_tensor("cst_bf", [128, 4, 128], BF16, kind="ExternalInput").ap()
    cst_f32 = nc.dram_tensor("cst_f32", [128, 128], F32, kind="ExternalInput").ap()
    mix = nc.dram_tensor("mix", [FPC, LT], BF16, kind="ExternalOutput").ap()
    sg = nc.dram_tensor("sg_scratch", [256, LT], BF16, kind="Internal").ap()
    with ExitStack() as es:
        p = Prog(nc)
        m = MixerB(nc, p, es, hT, w_att, w_conv, convw, gains, cst_bf, cst_f32, mix, sg)
        m.load_consts()
        m.load_w_all(w_att)
        p.barrier()
        for b in range(NBATCH):
            m.phase_proj_att(b)
            p.barrier()
            m.phase_attention(b, wload=(w_conv if b == NBATCH - 1 else None))
            p.barrier()
        for b in range(NBATCH):
            m.phase_conv(b)
            p.barrier()
        p.build()
    return nc


def emit_stats(nc, p, es, hs, part, tag="S"):
    sb = lambda name, shape, dt: es.enter_context(nc.sbuf_tensor(tag + name, shape, dt))
    x = sb("x", [128, 2, 4, 512], F32)
    sq = sb("sq", [128, 2, 4, 512], F32)
    ones = sb("ones", [128, 128], F32)
    pt = sb("pt", [1, 2, 512], F32)
    bank = [es.enter_context(nc.psum_tensor(tag + "bk%d" % i, [128, 512], F32)) for i in range(2)]
    bx, bsq, bpt, bbk = [Buf(), Buf()], [Buf(), Buf()], [Buf(), Buf()], [Buf(), Buf()]
    bones = Buf()
    p.op("pool", lambda e: e.memset(ones[:], 1.0), writes=[bones])
    hsv = hs.rearrange("(k q) t -> q k t", q=128)
    for i, (t0, tw) in enumerate(_tiles(LT, 512)):
        s = i % 2
        p.op("sp", lambda e, s=s, t0=t0, tw=tw: e.dma_start(out=x[:, s, :, 0:tw], in_=hsv[:, :, t0:t0 + tw]),
             writes=[bx[s]], dma=tag + "x%d" % s)
        p.op("act", lambda e, s=s, tw=tw: e.activation(out=sq[:, s, :, 0:tw], in_=x[:, s, :, 0:tw], func=AF.Square),
             reads=[bx[s]], writes=[bsq[s]])

        def fn(e, s=s, tw=tw):
            ins = None
            for k in range(4):
                ins = e.matmul(bank[s][:, 0:tw], lhsT=ones[:, :], rhs=sq[:, s, k, 0:tw], start=(k == 0), stop=(k == 3))
            return ins
        p.op("pe", fn, reads=[bsq[s], bones], writes=[bbk[s]])
        p.op("act", lambda e, s=s, tw=tw: e.activation(out=pt[0:1, s, 0:tw], in_=bank[s][0:1, 0:tw], func=AF.Copy),
             reads=[bbk[s]], writes=[bpt[s]])
        p.op("sp", lambda e, s=s, t0=t0, tw=tw: e.dma_start(out=part[0:1, t0:t0 + tw], in_=pt[0:1, s, 0:tw]),
             reads=[bpt[s]], dma=tag + "p%d" % s)


def build_S():
    nc = bass.Bass("TRN2", target_bir_lowering=False)
    hs = nc.dram_tensor("hs", [FPC, LT], F32, kind="ExternalInput").ap()
    part = nc.dram_tensor("part", [1, LT], F32, kind="ExternalOutput").ap()
    with ExitStack() as es:
        p = Prog(nc)
        emit_stats(nc, p, es, hs, part)
        p.barrier()
        p.build()
    return nc


def emit_norm(nc, p, es, hs, parts, gain, out, out_dt, tag="A"):
    sb = lambda name, shape, dt: es.enter_context(nc.sbuf_tensor(tag + name, shape, dt))
    x = sb("x", [128, 2, 4, 512], F32)
    y = sb("y", [128, 2, 4, 512], out_dt)
    pt = sb("pt", [8, 2, 512], F32)
    rs = sb("rs", [128, 2, 512], F32)
    o8 = sb("o8", [8, 128], F32)
    g = sb("g", [128, 4], F32)
    epsc = sb("eps", [128, 1], F32)
    bank = [es.enter_context(nc.psum_tensor(tag + "bk%d" % i, [128, 512], F32)) for i in range(2)]
    bx, by, bpt, brs, bbk = ([Buf(), Buf()] for _ in range(5))
    bc = Buf()
    p.op("pool", lambda e: e.memset(o8[:], 1.0 / D), writes=[bc])
    p.op("pool", lambda e: e.memset(epsc[:], EPS), writes=[bc])
    p.op("sp", lambda e: e.dma_start(out=g[:], in_=gain), writes=[bc], dma=tag + "g")
    hsv = hs.rearrange("(k q) t -> q k t", q=128)
    outv = out.rearrange("(k q) t -> q k t", q=128)
    for i, (t0, tw) in enumerate(_tiles(LT, 512)):
        s = i % 2
        p.op("sp", lambda e, s=s, t0=t0, tw=tw: e.dma_start(out=pt[:, s, 0:tw], in_=parts[:, t0:t0 + tw]),
             writes=[bpt[s]], dma=tag + "pt%d" % s)
        p.op("sp", lambda e, s=s, t0=t0, tw=tw: e.dma_start(out=x[:, s, :, 0:tw], in_=hsv[:, :, t0:t0 + tw]),
             writes=[bx[s]], dma=tag + "x%d" % s)
        p.op("pe", lambda e, s=s, tw=tw: e.matmul(bank[s][:, 0:tw], lhsT=o8[:, :], rhs=pt[:, s, 0:tw], start=True, stop=True),
             reads=[bpt[s], bc], writes=[bbk[s]])
        p.op("act", lambda e, s=s, tw=tw: e.activation(out=rs[:, s, 0:tw], in_=bank[s][:, 0:tw], func=AF.Ln, bias=epsc[:, 0:1]),
             reads=[bbk[s], bc], writes=[brs[s]])
        p.op("act", lambda e, s=s, tw=tw: e.activation(out=rs[:, s, 0:tw], in_=rs[:, s, 0:tw], func=AF.Exp, scale=-0.5),
             reads=[], writes=[brs[s]])
        for k in range(4):
            p.op("dve", lambda e, s=s, tw=tw, k=k: e.scalar_tensor_tensor(
                out=y[:, s, k, 0:tw], in0=x[:, s, k, 0:tw], scalar=g[:, k:k + 1], in1=rs[:, s, 0:tw],
                op0=ALU.mult, op1=ALU.mult),
                reads=[bx[s], brs[s], bc], writes=[by[s]])
        p.op("sp", lambda e, s=s, t0=t0, tw=tw: e.dma_start(out=outv[:, :, t0:t0 + tw], in_=y[:, s, :, 0:tw]),
             reads=[by[s]], dma=tag + "y%d" % s)


def build_A(final):
    nc = bass.Bass("TRN2", target_bir_lowering=False)
    hs = nc.dram_tensor("hs", [FPC, LT], F32, kind="ExternalInput").ap()
    parts = nc.dram_tensor("parts", [NCORES, LT], F32, kind="ExternalInput").ap()
    gain = nc.dram_tensor("gain", [128, 4], F32, kind="ExternalInput").ap()
    odt = F32 if final else BF16
    out = nc.dram_tensor("hn", [FPC, LT], odt, kind="ExternalOutput").ap()
    with ExitStack() as es:
        p = Prog(nc)
        emit_norm(nc, p, es, hs, parts, gain, out, odt)
        p.barrier()
        p.build()
    return nc


def emit_outproj(nc, p, es, mixT, wo, hs, hs_new, part, tag="C"):
    sb = lambda name, shape, dt: es.enter_context(nc.sbuf_tensor(tag + name, shape, dt))
    W = sb("W", [128, NCH, FPC], BF16)
    stg = sb("stg", [128, 2, 2, FPC], F32)
    ms = sb("ms", [128, 2, NCH, TT], BF16)
    x = sb("x", [128, 2, 4, TT], F32)
    xn = sb("xn", [128, 2, 4, TT], F32)
    sq = sb("sq", [128, 2, 4, TT], F32)
    ones = sb("ones", [128, 128], F32)
    pt = sb("pt", [1, 2, TT], F32)
    banks = [es.enter_context(nc.psum_tensor(tag + "bk%d" % i, [128, 512], F32)) for i in range(6)]
    bbk = [Buf() for _ in range(6)]
    bW, bones = Buf(), Buf()
    bstg, bms, bx, bxn, bsq, bpt = ([Buf(), Buf()] for _ in range(6))
    p.op("pool", lambda e: e.memset(ones[:], 1.0), writes=[bones])
    wv = wo.rearrange("(c q) n -> q c n", q=128)
    for st in range(NCH // 2):
        s = st % 2
        p.op("sp", lambda e, s=s, st=st: e.dma_start(out=stg[:, s, :, :], in_=wv[:, 2 * st:2 * st + 2, :]),
             writes=[bstg[s]], dma=tag + "st%d" % s)
        p.op("pool", lambda e, s=s, st=st: e.tensor_copy(out=W[:, 2 * st:2 * st + 2, :], in_=stg[:, s, :, :]),
             reads=[bstg[s]], writes=[bW])
    mv = mixT.rearrange("(c q) t -> q c t", q=128)
    hsv = hs.rearrange("(k q) t -> q k t", q=128)
    hnv = hs_new.rearrange("(k q) t -> q k t", q=128)
    q4 = NCH // 4
    tiles = _tiles(LT, TT)

    def load(i):
        t0, tw = tiles[i]
        s = i % 2
        for j in range(4):
            p.op("sp", lambda e, j=j, s=s, t0=t0, tw=tw: e.dma_start(out=ms[:, s, q4 * j:q4 * j + q4, 0:tw],
                                                                     in_=mv[:, q4 * j:q4 * j + q4, t0:t0 + tw]),
                 writes=[bms[s]], dma=tag + "ms%d" % s)
        p.op("sp", lambda e, s=s, t0=t0, tw=tw: e.dma_start(out=x[:, s, :, 0:tw], in_=hsv[:, :, t0:t0 + tw]),
             writes=[bx[s]], dma=tag + "x%d" % s)

    load(0)
    rr = 0
    for i, (t0, tw) in enumerate(tiles):
        s = i % 2
        if i + 1 < len(tiles):
            load(i + 1)
        for k in range(4):
            bk = rr % 4
            rr += 1

            def fn(e, s=s, tw=tw, k=k, bk=bk):
                ins = None
                for c in range(NCH):
                    ins = e.matmul(banks[bk][:, 0:tw], lhsT=W[:, c, 128 * k:128 * k + 128], rhs=ms[:, s, c, 0:tw],
                                   start=(c == 0), stop=(c == NCH - 1))
                return ins
            p.op("pe", fn, reads=[bW, bms[s]], writes=[bbk[bk]])
            p.op("dve", lambda e, s=s, tw=tw, k=k, bk=bk: e.tensor_tensor(out=xn[:, s, k, 0:tw], in0=banks[bk][:, 0:tw],
                                                                         in1=x[:, s, k, 0:tw], op=ALU.add),
                 reads=[bbk[bk], bx[s]], writes=[bxn[s]])
        p.op("act", lambda e, s=s, tw=tw: e.activation(out=sq[:, s, :, 0:tw], in_=xn[:, s, :, 0:tw], func=AF.Square),
             reads=[bxn[s]], writes=[bsq[s]])
        mb = 4 + s

        def fs(e, s=s, tw=tw, mb=mb):
            ins = None
            for k in range(4):
                ins = e.matmul(banks[mb][:, 0:tw], lhsT=ones[:, :], rhs=sq[:, s, k, 0:tw], start=(k == 0), stop=(k == 3))
            return ins
        p.op("pe", fs, reads=[bsq[s], bones], writes=[bbk[mb]])
        p.op("act", lambda e, s=s, tw=tw, mb=mb: e.activation(out=pt[0:1, s, 0:tw], in_=banks[mb][0:1, 0:tw], func=AF.Copy),
             reads=[bbk[mb]], writes=[bpt[s]])
        p.op("sp", lambda e, s=s, t0=t0, tw=tw: e.dma_start(out=part[0:1, t0:t0 + tw], in_=pt[0:1, s, 0:tw]),
             reads=[bpt[s]], dma=tag + "p%d" % s)
        p.op("sp", lambda e, s=s, t0=t0, tw=tw: e.dma_start(out=hnv[:, :, t0:t0 + tw], in_=xn[:, s, :, 0:tw]),
             reads=[bxn[s]], dma=tag + "o%d" % s)


def build_C():
    nc = bass.Bass("TRN2", target_bir_lowering=False)
    mixT = nc.dram_tensor("mixT", [D, LT], BF16, kind="ExternalInput").ap()
    wo = nc.dram_tensor("wo", [D, FPC], F32, kind="ExternalInput").ap()
    hs = nc.dram_tensor("hs", [FPC, LT], F32, kind="ExternalInput").ap()
    hs_new = nc.dram_tensor("hs_new", [FPC, LT], F32, kind="ExternalOutput").ap()
    part = nc.dram_tensor("part", [1, LT], F32, kind="ExternalOutput").ap()
    with ExitStack() as es:
        p = Prog(nc)
        emit_outproj(nc, p, es, mixT, wo, hs, hs_new, part)
        p.barrier()
        p.build()
    return nc


def _run(nc, in_maps):
    res = run_bass_kernel_spmd(nc, in_maps, core_ids=list(range(NCORES)))
    return res.results


def _w_in_core(w_in_l, c):
    DA = 2048
    cols = []
    for base in (0, DA, 2 * DA, 3 * DA):
        cols.append(w_in_l[:, base + 256 * c: base + 256 * c + 256])
    w_att = np.ascontiguousarray(np.concatenate(cols, axis=1))
    cols = []
    for base in (4 * DA, 5 * DA, 6 * DA, 7 * DA):
        cols.append(w_in_l[:, base + 256 * c: base + 256 * c + 256])
    w_conv = np.ascontiguousarray(np.concatenate(cols, axis=1))
    return w_att, w_conv


def _mix_row_perm():
    idx = np.empty(D, np.int64)
    for c in range(NCORES):
        for u in range(4):
            base = (2 * c + u) * 128 if u < 2 else 2048 + (2 * c + u - 2) * 128
            idx[c * 512 + u * 128: c * 512 + u * 128 + 128] = base + np.arange(128)
    return idx


_PROGS = {}


def _prog(name):
    if name not in _PROGS:
        _PROGS[name] = {"S": build_S, "A": lambda: build_A(False), "F": lambda: build_A(True),
                        "B": build_B, "C": build_C}[name]()
    return _PROGS[name]


def kernel(x, meta_tokens, norm_g, w_in, conv_w, attn_norm_g, conv_norm_g, w_out, final_norm_g):
    x = np.asarray(x, np.float32)
    depth = w_in.shape[0]
    hsT = np.zeros((D, LT), np.float32)
    for b in range(NBATCH):
        hsT[:, b * LP:b * LP + NMETA] = np.asarray(meta_tokens, np.float32).T
        hsT[:, b * LP + NMETA:b * LP + NMETA + SEQ] = x[b].T
    hs = [np.ascontiguousarray(hsT[FPC * c:FPC * (c + 1)]) for c in range(NCORES)]
    cbf, cf32 = _const_arrays()
    perm = _mix_row_perm()

    def colgain(gvec, c):
        return np.ascontiguousarray(np.asarray(gvec, np.float32)[FPC * c:FPC * (c + 1)].reshape(4, 128).T)

    r = _run(_prog("S"), [dict(hs=hs[c]) for c in range(NCORES)])
    parts = np.ascontiguousarray(np.concatenate([r[c]["part"] for c in range(NCORES)], axis=0))
    for l in range(depth):
        r = _run(_prog("A"), [dict(hs=hs[c], parts=parts, gain=colgain(norm_g[l], c)) for c in range(NCORES)])
        hT = np.ascontiguousarray(np.concatenate([r[c]["hn"] for c in range(NCORES)], axis=0))
        ims = []
        for c in range(NCORES):
            w_att, w_conv = _w_in_core(np.asarray(w_in[l], np.float32), c)
            cw = np.asarray(conv_w[l], np.float32)
            convw = np.concatenate([cw[:, (2 * c + gi) * 128:(2 * c + gi) * 128 + 128].T for gi in range(2)], axis=1)
            ag = np.asarray(attn_norm_g[l], np.float32)
            cg = np.asarray(conv_norm_g[l], np.float32)
            gains = np.stack([ag[(2 * c) * 128:(2 * c) * 128 + 128], ag[(2 * c + 1) * 128:(2 * c + 1) * 128 + 128],
                              cg[(2 * c) * 128:(2 * c) * 128 + 128], cg[(2 * c + 1) * 128:(2 * c + 1) * 128 + 128]], axis=1)
            ims.append(dict(hT=hT, w_att=w_att, w_conv=w_conv, convw=np.ascontiguousarray(convw),
                            gains=np.ascontiguousarray(gains), cst_bf=cbf, cst_f32=cf32))
        r = _run(_prog("B"), ims)
        mixT = np.ascontiguousarray(np.concatenate([r[c]["mix"] for c in range(NCORES)], axis=0))
        wo_l = np.asarray(w_out[l], np.float32)[perm]
        r = _run(_prog("C"), [dict(mixT=mixT, wo=np.ascontiguousarray(wo_l[:, FPC * c:FPC * (c + 1)]), hs=hs[c])
                              for c in range(NCORES)])
        hs = [r[c]["hs_new"] for c in range(NCORES)]
        parts = np.ascontiguousarray(np.concatenate([r[c]["part"] for c in range(NCORES)], axis=0))
    r = _run(_prog("F"), [dict(hs=hs[c], parts=parts, gain=colgain(final_norm_g, c)) for c in range(NCORES)])
    out = np.empty((NBATCH, SEQ, D), np.float32)
    for c in range(NCORES):
        o = r[c]["hn"]
        for b in range(NBATCH):
            out[b, :, FPC * c:FPC * (c + 1)] = o[:, b * LP + NMETA:b * LP + NMETA + SEQ].T
    return out
```
